# Optimizing a Trainium2 kernel written in Bass

```python
import math
import jax, jax.numpy as jnp
from jax import lax
import numpy as np

D_MODEL = 4096
BATCH = 2
SEQ = 4096
DEPTH = 4

N_MIXERS = 3
CONV_WIDTH = 31
RET_HEADS = 16
RET_QK_DIM = D_MODEL // RET_HEADS
RET_V_DIM = 2 * D_MODEL // RET_HEADS
RET_CHUNK = 128
ROPE_BASE = 10000.0
HGRN_HEAD_DIM = 128
HGRN_HEADS = D_MODEL // HGRN_HEAD_DIM
HGRN_CHUNK = 64
MLP_HIDDEN = 4 * D_MODEL
LN_EPS = 1e-5
RMS_EPS = 1e-6
MAX_POS_OFFSET = 1024

kernel_name = "interleaved_conv_retention_hgrn2_deepnorm"


def _layer_norm(u, g, b):
    uf = u.astype(jnp.float32)
    mu = jnp.mean(uf, axis=-1, keepdims=True)
    var = jnp.mean(jnp.square(uf - mu), axis=-1, keepdims=True)
    y = (uf - mu) * lax.rsqrt(var + LN_EPS)
    return (y * g.astype(jnp.float32) + b.astype(jnp.float32)).astype(u.dtype)


def _rms(u):
    uf = u.astype(jnp.float32)
    return uf * lax.rsqrt(jnp.mean(jnp.square(uf), axis=-1, keepdims=True) + RMS_EPS)


def _rotary(u, positions):
    half = u.shape[-1] // 2
    inv_freq = 1.0 / (ROPE_BASE ** jnp.linspace(0.0, 1.0, half, dtype=jnp.float32))
    ang = positions.astype(jnp.float32)[:, :, None, None] * inv_freq
    cos, sin = jnp.cos(ang), jnp.sin(ang)
    u1 = u[..., :half].astype(jnp.float32)
    u2 = u[..., half:].astype(jnp.float32)
    return jnp.concatenate([u1 * cos - u2 * sin, u2 * cos + u1 * sin], axis=-1).astype(u.dtype)


def _to_chunks(u, c):
    b, s = u.shape[:2]
    return u.reshape(b, s // c, c, u.shape[2], u.shape[3]).transpose(1, 0, 3, 2, 4)


def _from_chunks(u):
    n, b, h, c, d = u.shape
    return u.transpose(1, 0, 3, 2, 4).reshape(b, n * c, h, d)


def _conformer_conv(x, w_in, b_in, w_dw, b_dw, ln_g, ln_b, w_out, b_out):
    h = x @ w_in + b_in
    a, gate = jnp.split(h, 2, axis=-1)
    u = a * jax.nn.sigmoid(gate)
    u = lax.conv_general_dilated(
        u, w_dw[:, None, :], window_strides=(1,), padding=[(CONV_WIDTH - 1, 0)],
        dimension_numbers=("NWC", "WIO", "NWC"), feature_group_count=u.shape[-1]) + b_dw
    u = jax.nn.silu(_layer_norm(u, ln_g, ln_b))
    return u @ w_out + b_out


def _retention(x, positions, w_q, w_k, w_v, w_g, w_o):
    b, s, _ = x.shape
    dt = x.dtype
    q = (x @ w_q).reshape(b, s, RET_HEADS, RET_QK_DIM)
    k = (x @ w_k).reshape(b, s, RET_HEADS, RET_QK_DIM) * (RET_QK_DIM ** -0.5)
    v = (x @ w_v).reshape(b, s, RET_HEADS, RET_V_DIM)
    q, k = _rotary(q, positions), _rotary(k, positions)

    log_gamma = jnp.log1p(-(2.0 ** (-5.0 - jnp.arange(RET_HEADS, dtype=jnp.float32))))
    t = jnp.arange(RET_CHUNK, dtype=jnp.float32)
    diff = t[:, None] - t[None, :]
    intra = jnp.where(diff >= 0, jnp.exp(log_gamma[:, None, None] * jnp.maximum(diff, 0.0)), 0.0).astype(dt)
    q_decay = jnp.exp(log_gamma[:, None] * (t + 1.0)).astype(dt)
    k_decay = jnp.exp(log_gamma[:, None] * (RET_CHUNK - 1.0 - t)).astype(dt)
    chunk_decay = jnp.exp(log_gamma * RET_CHUNK).astype(dt)

    qc, kc, vc = _to_chunks(q, RET_CHUNK), _to_chunks(k, RET_CHUNK), _to_chunks(v, RET_CHUNK)

    def step(state, inp):
        qi, ki, vi = inp
        scores = jnp.einsum("bhtd,bhsd->bhts", qi, ki) * intra[None]
        o = jnp.einsum("bhts,bhsv->bhtv", scores, vi) + jnp.einsum(
            "bhtd,bhdv->bhtv", qi * q_decay[None, :, :, None], state)
        state = chunk_decay[None, :, None, None] * state + jnp.einsum(
            "bhsd,bhsv->bhdv", ki * k_decay[None, :, :, None], vi)
        return state, o

    state0 = jnp.zeros((b, RET_HEADS, RET_QK_DIM, RET_V_DIM), dt)
    _, o = lax.scan(step, state0, (qc, kc, vc))
    o = _rms(_from_chunks(o)).astype(dt).reshape(b, s, RET_HEADS * RET_V_DIM)
    o = o * jax.nn.silu(x @ w_g)
    return o @ w_o


def _hgrn2(x, lower_bound, w_q, w_f, w_i, w_g, norm_g, w_o):
    b, s, _ = x.shape
    dt = x.dtype
    q = jax.nn.silu(x @ w_q).reshape(b, s, HGRN_HEADS, HGRN_HEAD_DIM)
    z = (x @ w_f).astype(jnp.float32).reshape(b, s, HGRN_HEADS, HGRN_HEAD_DIM)
    lb = lower_bound.astype(jnp.float32).reshape(HGRN_HEADS, HGRN_HEAD_DIM)
    log_f = jnp.logaddexp(jnp.log(lb), jnp.log1p(-lb) + jax.nn.log_sigmoid(z))
    k = ((1.0 - lb) * jax.nn.sigmoid(-z)).astype(dt)
    i = (x @ w_i).reshape(b, s, HGRN_HEADS, HGRN_HEAD_DIM)

    qc, kc, ic = _to_chunks(q, HGRN_CHUNK), _to_chunks(k, HGRN_CHUNK), _to_chunks(i, HGRN_CHUNK)
    bc = jnp.cumsum(_to_chunks(log_f, HGRN_CHUNK), axis=3)
    causal = jnp.tril(jnp.ones((HGRN_CHUNK, HGRN_CHUNK), dtype=bool))

    def step(state, inp):
        qi, ki, ii, bi = inp
        pair_diff = bi[:, :, :, None, :] - bi[:, :, None, :, :]
        pair = jnp.where(causal[None, None, :, :, None],
                         jnp.exp(jnp.minimum(pair_diff, 0.0)), 0.0).astype(dt)
        scores = jnp.einsum("bhtk,bhsk,bhtsk->bhts", qi, ki, pair)
        b_last = bi[:, :, -1:, :]
        o = jnp.einsum("bhts,bhsv->bhtv", scores, ii) + jnp.einsum(
            "bhtk,bhkv->bhtv", qi * jnp.exp(bi).astype(dt), state)
        state = jnp.exp(b_last[:, :, 0, :]).astype(dt)[..., None] * state + jnp.einsum(
            "bhsk,bhsv->bhkv", ki * jnp.exp(b_last - bi).astype(dt), ii)
        return state, o

    state0 = jnp.zeros((b, HGRN_HEADS, HGRN_HEAD_DIM, HGRN_HEAD_DIM), dt)
    _, o = lax.scan(step, state0, (qc, kc, ic, bc))
    o = _from_chunks(o).reshape(b, s, HGRN_HEADS * HGRN_HEAD_DIM)
    o = (_rms(o) * norm_g.astype(jnp.float32)).astype(dt) * jax.nn.silu(x @ w_g)
    return o @ w_o


def _sq_relu_mlp(x, w1, w2):
    return jnp.square(jax.nn.relu(x @ w1)) @ w2


def setup_inputs(seed: int = 0) -> dict:
    key = jax.random.key(seed)
    ks = iter(jax.random.split(key, 32))
    d = D_MODEL
    n_a = (DEPTH + 2) // 3
    n_b = (DEPTH + 1) // 3
    n_c = DEPTH // 3
    beta = (8.0 * DEPTH) ** -0.25

    def nrm(shape, scale):
        return jax.random.normal(next(ks), shape, jnp.float32) * scale

    x = nrm((BATCH, SEQ, d), 1.0)
    positions = (jax.random.randint(next(ks), (BATCH, 1), 0, MAX_POS_OFFSET)
                 + jnp.arange(SEQ)[None, :]).astype(jnp.int32)
    return {
        "x": x,
        "positions": positions,
        "conv_w_in": nrm((n_a, d, 2 * d), d ** -0.5),
        "conv_b_in": nrm((n_a, 2 * d), 0.02),
        "conv_w_dw": nrm((n_a, CONV_WIDTH, d), CONV_WIDTH ** -0.5),
        "conv_b_dw": nrm((n_a, d), 0.02),
        "conv_ln_g": 1.0 + nrm((n_a, d), 0.02),
        "conv_ln_b": nrm((n_a, d), 0.02),
        "conv_w_out": nrm((n_a, d, d), beta * d ** -0.5),
        "conv_b_out": nrm((n_a, d), 0.02),
        "ret_w_q": nrm((n_b, d, RET_HEADS * RET_QK_DIM), d ** -0.5),
        "ret_w_k": nrm((n_b, d, RET_HEADS * RET_QK_DIM), d ** -0.5),
        "ret_w_v": nrm((n_b, d, RET_HEADS * RET_V_DIM), beta * d ** -0.5),
        "ret_w_g": nrm((n_b, d, RET_HEADS * RET_V_DIM), d ** -0.5),
        "ret_w_o": nrm((n_b, RET_HEADS * RET_V_DIM, d), beta * (RET_HEADS * RET_V_DIM) ** -0.5),
        "hgrn_lower_bounds": nrm((DEPTH, HGRN_HEADS * HGRN_HEAD_DIM), 0.1),
        "hgrn_w_q": nrm((n_c, d, HGRN_HEADS * HGRN_HEAD_DIM), d ** -0.5),
        "hgrn_w_f": nrm((n_c, d, HGRN_HEADS * HGRN_HEAD_DIM), d ** -0.5),
        "hgrn_w_i": nrm((n_c, d, HGRN_HEADS * HGRN_HEAD_DIM), beta * d ** -0.5),
        "hgrn_w_g": nrm((n_c, d, HGRN_HEADS * HGRN_HEAD_DIM), d ** -0.5),
        "hgrn_norm_g": 1.0 + nrm((n_c, HGRN_HEADS * HGRN_HEAD_DIM), 0.02),
        "hgrn_w_o": nrm((n_c, HGRN_HEADS * HGRN_HEAD_DIM, d), beta * d ** -0.5),
        "mlp_w1": nrm((DEPTH, d, MLP_HIDDEN), beta * d ** -0.5),
        "mlp_w2": nrm((DEPTH, MLP_HIDDEN, d), beta * MLP_HIDDEN ** -0.5),
        "ln_mix_g": 1.0 + nrm((DEPTH, d), 0.02),
        "ln_mix_b": nrm((DEPTH, d), 0.02),
        "ln_mlp_g": 1.0 + nrm((DEPTH, d), 0.02),
        "ln_mlp_b": nrm((DEPTH, d), 0.02),
    }


def reference(x, positions, conv_w_in, conv_b_in, conv_w_dw, conv_b_dw, conv_ln_g, conv_ln_b,
              conv_w_out, conv_b_out, ret_w_q, ret_w_k, ret_w_v, ret_w_g, ret_w_o,
              hgrn_lower_bounds, hgrn_w_q, hgrn_w_f, hgrn_w_i, hgrn_w_g, hgrn_norm_g, hgrn_w_o,
              mlp_w1, mlp_w2, ln_mix_g, ln_mix_b, ln_mlp_g, ln_mlp_b):
    alpha = (2.0 * DEPTH) ** 0.25
    lb_all = jnp.cumsum(jax.nn.softmax(hgrn_lower_bounds.astype(jnp.float32), axis=0), axis=0)
    lb_all = lb_all - lb_all[0:1]
    h = x
    for i in range(DEPTH):
        kind, j = i % N_MIXERS, i // N_MIXERS
        if kind == 0:
            y = _conformer_conv(h, conv_w_in[j], conv_b_in[j], conv_w_dw[j], conv_b_dw[j],
                                conv_ln_g[j], conv_ln_b[j], conv_w_out[j], conv_b_out[j])
        elif kind == 1:
            y = _retention(h, positions, ret_w_q[j], ret_w_k[j], ret_w_v[j], ret_w_g[j], ret_w_o[j])
        else:
            y = _hgrn2(h, lb_all[i], hgrn_w_q[j], hgrn_w_f[j], hgrn_w_i[j], hgrn_w_g[j],
                       hgrn_norm_g[j], hgrn_w_o[j])
        h = _layer_norm(alpha * h + y, ln_mix_g[i], ln_mix_b[i])
        h = _layer_norm(alpha * h + _sq_relu_mlp(h, mlp_w1[i], mlp_w2[i]), ln_mlp_g[i], ln_mlp_b[i])
    return h
```

```python
import contextlib
import numpy as np
import ml_dtypes
import concourse.bass as bass
import concourse.mybir as mybir
from concourse.bass_utils import run_bass_kernel_spmd

F32 = mybir.dt.float32
BF16 = mybir.dt.bfloat16
I32 = mybir.dt.int32
ALU = mybir.AluOpType
AF = mybir.ActivationFunctionType

P = 128
T = 512
D = 4096
DC = 32
DEPTH = 4
ALPHA = (2.0 * DEPTH) ** 0.25
LN_EPS = 1e-5
RMS_EPS = 1e-6
CONV_W = 31
HALO = CONV_W - 1
RET_H = 16
HG_H = 32
SEM_LIMIT = 12000


class Buf:
    __slots__ = ("name", "w", "r")

    def __init__(self, name):
        self.name = name
        self.w = {}
        self.r = {}


class Eng:
    def __init__(self, k, name, obj, inorder):
        self.k = k
        self.name = name
        self.obj = obj
        self.inorder = inorder
        self.sem = None
        self.semkey = None
        self.n = 0
        self.seen = {}
        self.nsem = 0
        self.pending = False


class K:
    def __init__(self, nc, stack):
        self.nc = nc
        self.stack = stack
        self.PE = Eng(self, "pe", nc.tensor, True)
        self.ACT = Eng(self, "act", nc.scalar, False)
        self.DVE = Eng(self, "dve", nc.vector, False)
        self.POOL = Eng(self, "pool", nc.gpsimd, False)
        self.SP = Eng(self, "sp", nc.sync, True)
        self.nchan = 0
        self.nbuf = 0

    def buf(self, name=None):
        self.nbuf += 1
        return Buf(name or f"b{self.nbuf}")

    def sbuf(self, name, shape, dt):
        return self.stack.enter_context(self.nc.sbuf_tensor(name, shape, dt))

    def psum(self, name, shape, dt):
        return self.stack.enter_context(self.nc.psum_tensor(name, shape, dt))

    def _newsem(self, e):
        e.nsem += 1
        e.sem = self.stack.enter_context(self.nc.semaphore(f"s_{e.name}_{e.nsem}"))
        e.semkey = f"{e.name}#{e.nsem}"
        e.n = 0

    def _wait(self, e, deps):
        for key, (sem, val) in deps.items():
            if e.inorder and key.startswith(e.name + "#"):
                continue
            if e.seen.get(key, 0) < val:
                e.obj.wait_ge(sem, val)
                e.seen[key] = val

    def op(self, e, fn, reads=(), writes=(), mark=True):
        deps = {}
        for b in reads:
            for kk, v in b.w.items():
                if deps.get(kk, (None, 0))[1] < v[1]:
                    deps[kk] = v
        for b in writes:
            for dd in (b.w, b.r):
                for kk, v in dd.items():
                    if deps.get(kk, (None, 0))[1] < v[1]:
                        deps[kk] = v
        self._wait(e, deps)
        if e.sem is None or (e.n >= SEM_LIMIT and not e.pending):
            self._newsem(e)
        ins = fn()
        if mark:
            e.n += 1
            ins.then_inc(e.sem, 1)
            d = (e.sem, e.n)
            e.pending = False
        else:
            d = (e.sem, e.n + 1)
            e.pending = True
        kk = e.semkey
        for b in reads:
            b.r[kk] = d
        for b in writes:
            b.w = {kk: d}
            b.r = {}
        return ins

    def chan(self):
        self.nchan += 1
        sem = self.stack.enter_context(self.nc.semaphore(f"s_ch{self.nchan}"))
        return [sem, 0, f"ch{self.nchan}"]

    def dma(self, e, ch, out, in_, reads=(), writes=()):
        deps = {}
        for b in reads:
            for kk, v in b.w.items():
                if deps.get(kk, (None, 0))[1] < v[1]:
                    deps[kk] = v
        for b in writes:
            for dd in (b.w, b.r):
                for kk, v in dd.items():
                    if deps.get(kk, (None, 0))[1] < v[1]:
                        deps[kk] = v
        if ch[1] > 0:
            deps[ch[2]] = (ch[0], ch[1])
        self._wait(e, deps)
        ins = e.obj.dma_start(out=out, in_=in_)
        ch[1] += 16
        ins.then_inc(ch[0], 16)
        d = (ch[0], ch[1])
        for b in reads:
            b.r[ch[2]] = d
        for b in writes:
            b.w = {ch[2]: d}
            b.r = {}
        return ins

    def scope_end(self, bufs):
        deps = {}
        for b in bufs:
            for dd in (b.w, b.r):
                for kk, v in dd.items():
                    if deps.get(kk, (None, 0))[1] < v[1]:
                        deps[kk] = v
        for e in (self.PE, self.ACT, self.DVE, self.POOL, self.SP):
            self._wait(e, deps)

    def finish(self, chans):
        for ch in chans:
            if ch[1] > 0:
                self.nc.sync.wait_ge(ch[0], ch[1])


class Rot:
    def __init__(self, items):
        self.items = items
        self.i = 0

    def next(self):
        it = self.items[self.i % len(self.items)]
        self.i += 1
        return it


class WStream:
    NRING = 3
    BPT = 128

    def __init__(self, k, plan1, nt):
        self.k = k
        self.plan1 = plan1
        self.n1 = len(plan1)
        self.total = self.n1 * nt
        nc = k.nc
        self.scr = [nc.dram_tensor(f"wscr{i}", [self.BPT, P, 4096], BF16).ap()
                    for i in range((self.n1 + self.BPT - 1) // self.BPT)]
        self.ring = [k.sbuf(f"wring{i}", [P, 4096], BF16) for i in range(self.NRING)]
        self.ring_b = [k.buf(f"wring{i}") for i in range(self.NRING)]
        self.ring_ch = [k.chan() for i in range(self.NRING)]
        self.loaded = 0
        self.i = 0

    def scr_blk(self, j):
        return self.scr[j // self.BPT][j % self.BPT]

    def prologue(self):
        k, nc = self.k, self.k.nc
        NST = 4
        with contextlib.ExitStack() as ms:
            stage = [ms.enter_context(nc.sbuf_tensor(f"wstage{i}", [P, 2048], F32)) for i in range(NST)]
            stage_b = [k.buf(f"wstage{i}") for i in range(NST)]
            stage_ch = [k.chan() for i in range(NST)]
            st_ch = [k.chan() for i in range(self.NRING)]
            nh = 2 * self.n1
            loaded = [0]

            def load():
                hj = loaded[0]
                j, hf = hj // 2, hj % 2
                key, ap, a, b = self.plan1[j]
                sl = hj % NST
                ah = a // 2
                dst = stage[sl][:, :].rearrange("p (a b) -> p a b", a=ah)
                k.dma(k.SP, stage_ch[sl], dst, ap[:, hf * ah:(hf + 1) * ah, :], writes=[stage_b[sl]])
                loaded[0] += 1

            engs = [k.ACT, k.POOL, k.DVE]
            for j in range(self.n1):
                r = j % self.NRING
                for hf in range(2):
                    hj = 2 * j + hf
                    while loaded[0] <= hj:
                        load()
                    sl = hj % NST
                    dst = self.ring[r][:, hf * 2048:(hf + 1) * 2048]
                    src = stage[sl][:, :]
                    e = engs[hj % 3]
                    if e is k.ACT:
                        k.op(e, lambda: nc.scalar.activation(out=dst, in_=src, func=AF.Copy), reads=[stage_b[sl]], writes=[self.ring_b[r]])
                    elif e is k.POOL:
                        k.op(e, lambda: nc.gpsimd.tensor_copy(out=dst, in_=src), reads=[stage_b[sl]], writes=[self.ring_b[r]])
                    else:
                        k.op(e, lambda: nc.vector.tensor_copy(out=dst, in_=src), reads=[stage_b[sl]], writes=[self.ring_b[r]])
                    while loaded[0] < min(nh, hj + NST):
                        load()
                k.dma(k.SP, st_ch[r], self.scr_blk(j), self.ring[r][:, :], reads=[self.ring_b[r]])
            k.finish(st_ch)
            k.scope_end(stage_b + self.ring_b)

    def _load(self):
        jj = self.loaded
        if jj >= self.total:
            return
        r = jj % self.NRING
        self.k.dma(self.k.SP, self.ring_ch[r], self.ring[r][:, :], self.scr_blk(jj % self.n1), writes=[self.ring_b[r]])
        self.loaded += 1

    def next(self, key):
        jj = self.i
        pkey, ap, a, b = self.plan1[jj % self.n1]
        assert pkey == key, (pkey, key, jj)
        while self.loaded <= jj:
            self._load()
        r = jj % self.NRING
        self.i += 1
        return self.ring[r][:, :].rearrange("p (a b) -> p a b", a=a), self.ring_b[r]

    def prefetch(self):
        while self.loaded < min(self.total, self.i + self.NRING - 1):
            self._load()


def wplan_layer(W, li, plan, DI=lambda x: x, mixer=True, mlp=True):
    kind, j = li % 3, li // 3

    def colblocks(name, w2d, n_out_chunks, col0=0):
        v = w2d.rearrange("(c p) n -> p c n", p=P)
        for oc in range(n_out_chunks):
            plan.append((name, v[:, :, col0 + oc * P: col0 + (oc + 1) * P], 32, P))

    if not mixer:
        pass
    elif kind == 0:
        w_in = W["conv_w_in"][j]
        vin = w_in.rearrange("(c p) n -> p c n", p=P)
        for oc in range(DC):
            plan.append(("cin_a", vin[:, :, oc * P:(oc + 1) * P], 32, P))
            plan.append(("cin_g", vin[:, :, D + oc * P: D + (oc + 1) * P], 32, P))
        colblocks("cout", W["conv_w_out"][j], DC)
    elif kind == 1:
        wq, wk, wv, wg, wo = (W["ret_w_q"][j], W["ret_w_k"][j], W["ret_w_v"][j], W["ret_w_g"][j], W["ret_w_o"][j])
        vq = wq.rearrange("(c p) n -> p c n", p=P)
        vk = wk.rearrange("(c p) n -> p c n", p=P)
        vv = wv.rearrange("(c p) n -> p c n", p=P)
        vg = wg.rearrange("(c p) n -> p c n", p=P)
        for h in range(RET_H):
            for cc in range(2):
                plan.append(("rq", vq[:, :, (2 * h + cc) * P:(2 * h + cc + 1) * P], 32, P))
            for cc in range(2):
                plan.append(("rk", vk[:, :, (2 * h + cc) * P:(2 * h + cc + 1) * P], 32, P))
            for cc in range(4):
                plan.append(("rv", vv[:, :, (4 * h + cc) * P:(4 * h + cc + 1) * P], 32, P))
            for cc in range(4):
                plan.append(("rg", vg[:, :, (4 * h + cc) * P:(4 * h + cc + 1) * P], 32, P))
            vo = wo[h * 512:(h + 1) * 512, :].rearrange("(c p) n -> p c n", p=P)
            for g in range(4):
                plan.append(("ro", vo[:, :, g * 1024:(g + 1) * 1024], 4, 1024))
    else:
        wq, wf, wi, wg, wo = (W["hgrn_w_q"][j], W["hgrn_w_f"][j], W["hgrn_w_i"][j], W["hgrn_w_g"][j], W["hgrn_w_o"][j])
        vq = wq.rearrange("(c p) n -> p c n", p=P)
        vf = wf.rearrange("(c p) n -> p c n", p=P)
        vi = wi.rearrange("(c p) n -> p c n", p=P)
        for h in range(HG_H):
            plan.append(("hq", vq[:, :, h * P:(h + 1) * P], 32, P))
            plan.append(("hf", vf[:, :, h * P:(h + 1) * P], 32, P))
            plan.append(("hi", vi[:, :, h * P:(h + 1) * P], 32, P))
        vg = wg.rearrange("(c p) n -> p c n", p=P)
        for h4 in range(8):
            for hh in range(4):
                oc = h4 * 4 + hh
                plan.append(("hg", vg[:, :, oc * P:(oc + 1) * P], 32, P))
            vo = wo[h4 * 512:(h4 + 1) * 512, :].rearrange("(c p) n -> p c n", p=P)
            for g in range(4):
                plan.append(("ho", vo[:, :, g * 1024:(g + 1) * 1024], 4, 1024))
    if not mlp:
        return
    w1 = W["mlp_w1"][DI(li)].rearrange("(c p) n -> p c n", p=P)
    w2 = W["mlp_w2"][DI(li)]
    for hb in range(16):
        for jj in range(8):
            jn = hb * 8 + jj
            plan.append(("m1", w1[:, :, jn * P:(jn + 1) * P], 32, P))
        v2 = w2[hb * 1024:(hb + 1) * 1024, :].rearrange("(c p) n -> p c n", p=P)
        for g in range(8):
            plan.append(("m2", v2[:, :, g * 512:(g + 1) * 512], 8, 512))


WEIGHT_NAMES = {
    0: ["conv_w_in", "conv_b_in", "conv_w_dw", "conv_b_dw", "conv_ln_g", "conv_ln_b", "conv_w_out", "conv_b_out"],
    1: ["ret_w_q", "ret_w_k", "ret_w_v", "ret_w_g", "ret_w_o"],
    2: ["hgrn_w_q", "hgrn_w_f", "hgrn_w_i", "hgrn_w_g", "hgrn_norm_g", "hgrn_w_o"],
}
SHAPES = {
    "conv_w_in": [2, D, 2 * D], "conv_b_in": [2, 2 * D], "conv_w_dw": [2, CONV_W, D], "conv_b_dw": [2, D],
    "conv_ln_g": [2, D], "conv_ln_b": [2, D], "conv_w_out": [2, D, D], "conv_b_out": [2, D],
    "ret_w_q": [1, D, D], "ret_w_k": [1, D, D], "ret_w_v": [1, D, 2 * D], "ret_w_g": [1, D, 2 * D],
    "ret_w_o": [1, 2 * D, D],
    "hgrn_lower_bounds": [DEPTH, D], "hgrn_w_q": [1, D, D], "hgrn_w_f": [1, D, D], "hgrn_w_i": [1, D, D],
    "hgrn_w_g": [1, D, D], "hgrn_norm_g": [1, D], "hgrn_w_o": [1, D, D],
    "mlp_w1": [DEPTH, D, 4 * D], "mlp_w2": [DEPTH, 4 * D, D],
    "ln_mix_g": [DEPTH, D], "ln_mix_b": [DEPTH, D], "ln_mlp_g": [DEPTH, D], "ln_mlp_b": [DEPTH, D],
}


RET_LG = [float(np.log1p(-(2.0 ** (-5.0 - h)))) for h in range(RET_H)]


def host_consts():
    c = {}
    c["ident_f"] = np.eye(P, dtype=np.float32)
    c["ident_b"] = np.eye(P, dtype=np.float32).astype(ml_dtypes.bfloat16)
    c["ones_f"] = np.ones((P, P), dtype=np.float32)
    t = np.arange(P, dtype=np.float64)
    diff = t[None, :] - t[:, None]
    c["diffT"] = np.maximum(diff, 0.0).astype(np.float32)
    c["causT"] = (diff >= 0).astype(np.float32)
    c["tp1"] = np.ascontiguousarray(np.broadcast_to(np.tile(t + 1.0, T // P)[None, :], (P, T))).astype(np.float32)
    c["pi"] = np.full((P, 16), np.pi, dtype=np.float32)
    lg = np.array(RET_LG, dtype=np.float64)
    c["ret_kdec"] = np.ascontiguousarray(np.exp(lg[None, :] * (P - 1.0 - t[:, None]))).astype(np.float32)
    c["rev"] = np.ascontiguousarray(np.broadcast_to((P - 1.0 - t).reshape(P, 1), (P, 16))).astype(np.float32)
    inv_freq = (1.0 / (10000.0 ** np.linspace(0.0, 1.0, P, dtype=np.float32))).astype(np.float32)
    c["inv_freq"] = np.ascontiguousarray(np.broadcast_to(inv_freq.reshape(P, 1), (P, 16))).astype(np.float32)
    return c


CONST_SHAPES = {
    "ident_f": ([P, P], F32), "ident_b": ([P, P], BF16), "ones_f": ([P, P], F32),
    "diffT": ([P, P], F32), "causT": ([P, P], F32), "tp1": ([P, T], F32), "pi": ([P, 16], F32), "ret_kdec": ([P, RET_H], F32), "rev": ([P, 16], F32),
    "inv_freq": ([P, 16], F32),
}


def build(nt, layers, do_mixer=True, do_mlp=True, dmap=None, dn=DEPTH, dbg=(), first1=False):
    S = nt * T
    DI = (lambda li: dmap[li]) if dmap else (lambda li: li)
    nc = bass.Bass("TRN2", target_bir_lowering=False)
    consts = host_consts()
    kinds = sorted(set(li % 3 for li in layers)) if do_mixer else []
    x_d = nc.dram_tensor("x", [S, D], F32, kind="ExternalInput").ap()
    pos_d = nc.dram_tensor("positions", [1, S], I32, kind="ExternalInput").ap()
    out_d = nc.dram_tensor("out", [S, D], F32, kind="ExternalOutput").ap()
    W = {}
    names = ["mlp_w1", "mlp_w2", "ln_mix_g", "ln_mix_b", "ln_mlp_g", "ln_mlp_b", "hgrn_lower_bounds"]
    for kd in kinds:
        names += WEIGHT_NAMES[kd]
    for nm in names:
        shp = list(SHAPES[nm])
        if nm.startswith("mlp_") or nm.startswith("ln_"):
            shp[0] = dn
        elif first1 and nm != "hgrn_lower_bounds":
            shp[0] = 1
        W[nm] = nc.dram_tensor(nm, shp, F32, kind="ExternalInput").ap()
    C = {}
    for nm, (shp, dt) in CONST_SHAPES.items():
        C[nm] = nc.dram_tensor("c_" + nm, shp, dt, kind="ExternalInput").ap()
    ret_state_d = nc.dram_tensor("ret_state", [RET_H, P, 1024], F32).ap() if 1 in kinds else None
    hg_state_d = nc.dram_tensor("hg_state", [HG_H, P, P], F32).ap() if 2 in kinds else None
    hg_o_d = nc.dram_tensor("hg_o", [HG_H, P, T], BF16).ap() if 2 in kinds else None

    plan1 = []
    for li in layers:
        wplan_layer(W, li, plan1, DI, do_mixer, do_mlp)
    if not do_mixer:
        plan1 = [b for b in plan1 if b[0] in ("m1", "m2")]
    if not do_mlp:
        plan1 = [b for b in plan1 if b[0] not in ("m1", "m2")]

    with contextlib.ExitStack() as stack:
        k = K(nc, stack)
        PE, ACT, DVE, POOL, SP = k.PE, k.ACT, k.DVE, k.POOL, k.SP
        ws = WStream(k, plan1, nt)
        X32 = k.sbuf("X32", [P, DC, T], F32)
        XB = k.sbuf("XB", [P, DC, T], BF16)
        X32b = [k.buf(f"X32_{c}") for c in range(DC)]
        XBb = [k.buf(f"XB_{c}") for c in range(DC)]
        IO = k.sbuf("IO", [P, T], F32)
        IOb = [k.buf(f"IO{g}") for g in range(2)]
        io_ch = [k.chan(), k.chan()]
        out_ch = [k.chan(), k.chan()]
        misc_ch = k.chan()
        st_ch = [k.chan(), k.chan()]
        chans = io_ch + out_ch + [misc_ch] + st_ch
        pbank = [k.psum(f"ps{i}", [P, T], F32) for i in range(7)]
        psT = k.psum("psT", [P, 2 * T], BF16)
        _pb = k.buf("psT")
        psTb = [_pb, _pb]
        pbuf = [k.buf(f"ps{i}") for i in range(7)]
        pacc = Rot([(pbank[i], pbuf[i]) for i in range(4)])
        ps_s1, ps_s2 = (pbank[4], pbuf[4]), (pbank[5], pbuf[5])
        pmisc = Rot([(pbank[6], pbuf[6])])
        cst = {}
        cstb = k.buf("consts")
        for nm, (shp, dt) in CONST_SHAPES.items():
            cst[nm] = k.sbuf("k_" + nm, shp, dt)
            if "noconst" in dbg or ("nosmall" in dbg and shp[1] == 1):
                continue
            k.dma(SP, misc_ch, cst[nm][:, :], C[nm], writes=[cstb])

        def load_vecT(dst, col0, src2d, nrows):
            for r0 in range(0, nrows, P):
                rows = min(P, nrows - r0)
                k.dma(SP, io_ch[0], IO[:rows, 0:P], src2d[r0:r0 + rows, :], writes=[IOb[0]])
                bank, bb = pmisc.next()
                k.op(PE, lambda: nc.tensor.transpose(out=bank[:, 0:rows], in_=IO[:rows, 0:P], identity=cst["ident_f"][:rows, :rows]),
                     reads=[IOb[0], cstb], writes=[bb])
                k.op(DVE, lambda: nc.vector.tensor_copy(out=dst[:, col0 + r0:col0 + r0 + rows], in_=bank[:, 0:rows]),
                     reads=[bb], writes=[cstb])

        LNP = {}
        lnp_t = k.sbuf("lnp", [P, len(layers) * 2 * 4 * DC], F32)
        for n_l, li in enumerate(layers):
            for n_s, pre in enumerate(("ln_mix", "ln_mlp")):
                base = ((n_l * 2 + n_s) * 4) * DC
                LNP[(pre, li)] = base
                if "nolnp" in dbg:
                    continue
                load_vecT(lnp_t, base, W[pre + "_g"][DI(li)].rearrange("(c p) -> c p", p=P), DC)
                load_vecT(lnp_t, base + DC, W[pre + "_b"][DI(li)].rearrange("(c p) -> c p", p=P), DC)
                k.op(DVE, lambda: nc.vector.tensor_scalar_mul(out=lnp_t[:, base + 2 * DC:base + 4 * DC], in0=lnp_t[:, base:base + 2 * DC],
                                                             scalar1=ALPHA), reads=[cstb], writes=[cstb])
                LNP[(pre, li)] = base
        tmpA = Rot([(k.sbuf(f"tmpA{i}", [P, T], F32), k.buf(f"tmpA{i}")) for i in range(2)])
        tmpB = Rot([(k.sbuf(f"tmpB{i}", [P, T], F32), k.buf(f"tmpB{i}")) for i in range(2)])
        stat_m = k.sbuf("stat_m", [P, T], F32)
        stat_r = k.sbuf("stat_r", [P, T], F32)
        stat_b = k.buf("stat")

        def proj(key, kc, rhs_ap, rhs_bufs, wsl=None, col0=0, bankrot=None):
            if wsl is None:
                wap, wb = ws.next(key)
            else:
                wap, wb = wsl
            bank, bb = (bankrot or pacc).next()
            for c in range(kc):
                k.op(PE, lambda: nc.tensor.matmul(bank[:, :], lhsT=wap[:, c, col0:col0 + P], rhs=rhs_ap(c),
                                                  start=(c == 0), stop=(c == kc - 1)),
                     reads=[wb, rhs_bufs(c)], writes=[bb], mark=(c == kc - 1))
            ws.prefetch()
            return bank, bb

        def stats_add(st, src_ap, src_buf, total):
            sq, sqb = tmpB.next()
            k.op(ACT, lambda: nc.scalar.activation(out=sq[:, :], in_=src_ap, func=AF.Square), reads=[src_buf], writes=[sqb])
            first, last = st["n"] == 0, st["n"] == total - 1
            k.op(PE, lambda: nc.tensor.matmul(ps_s1[0][:, :], lhsT=cst["ones_f"][:, :], rhs=src_ap, start=first, stop=last),
                 reads=[src_buf, cstb], writes=[ps_s1[1]], mark=True)
            k.op(PE, lambda: nc.tensor.matmul(ps_s2[0][:, :], lhsT=cst["ones_f"][:, :], rhs=sq[:, :], start=first, stop=last),
                 reads=[sqb, cstb], writes=[ps_s2[1]], mark=True)
            st["n"] += 1

        def stats_finish(nfeat, eps, mean=True):
            inv = 1.0 / nfeat
            tq, tqb = tmpA.next()
            if mean:
                k.op(DVE, lambda: nc.vector.tensor_scalar_mul(out=stat_m[:, :], in0=ps_s1[0][:, :], scalar1=inv),
                     reads=[ps_s1[1]], writes=[stat_b])
                k.op(DVE, lambda: nc.vector.tensor_tensor(out=tq[:, :], in0=stat_m[:, :], in1=stat_m[:, :], op=ALU.mult),
                     reads=[stat_b], writes=[tqb])
                k.op(DVE, lambda: nc.vector.scalar_tensor_tensor(out=tq[:, :], in0=ps_s2[0][:, :], scalar=inv, in1=tq[:, :],
                                                                 op0=ALU.mult, op1=ALU.subtract),
                     reads=[ps_s2[1], tqb], writes=[tqb])
                k.op(DVE, lambda: nc.vector.tensor_scalar_add(out=tq[:, :], in0=tq[:, :], scalar1=eps), reads=[tqb], writes=[tqb])
                k.op(DVE, lambda: nc.vector.reciprocal(out=tq[:, :], in_=tq[:, :]), reads=[tqb], writes=[tqb])
                k.op(ACT, lambda: nc.scalar.activation(out=stat_r[:, :], in_=tq[:, :], func=AF.Sqrt), reads=[tqb], writes=[stat_b])
                k.op(DVE, lambda: nc.vector.scalar_tensor_tensor(out=stat_m[:, :], in0=stat_m[:, :], scalar=-1.0, in1=stat_r[:, :],
                                                                 op0=ALU.mult, op1=ALU.mult),
                     reads=[stat_b], writes=[stat_b])
            else:
                k.op(DVE, lambda: nc.vector.tensor_scalar(out=tq[:, :], in0=ps_s2[0][:, :], scalar1=inv, scalar2=eps,
                                                          op0=ALU.mult, op1=ALU.add),
                     reads=[ps_s2[1], ps_s1[1]], writes=[tqb])
                k.op(DVE, lambda: nc.vector.reciprocal(out=tq[:, :], in_=tq[:, :]), reads=[tqb], writes=[tqb])
                k.op(ACT, lambda: nc.scalar.activation(out=stat_r[:, :], in_=tq[:, :], func=AF.Sqrt), reads=[tqb], writes=[stat_b])

        def layer_norm(pre, li, final):
            base = LNP[(pre, li)]
            st = {"n": 0}
            for c in range(DC):
                stats_add(st, X32[:, c, :], X32b[c], DC)
            stats_finish(D, LN_EPS)
            for c in range(DC):
                t1, t1b = tmpA.next()
                k.op(DVE, lambda: nc.vector.tensor_tensor(out=t1[:, :], in0=X32[:, c, :], in1=stat_r[:, :], op=ALU.mult),
                     reads=[X32b[c], stat_b], writes=[t1b])
                k.op(DVE, lambda: nc.vector.tensor_tensor(out=t1[:, :], in0=t1[:, :], in1=stat_m[:, :], op=ALU.add),
                     reads=[t1b, stat_b], writes=[t1b])
                go, bo = (base, base + DC) if final else (base + 2 * DC, base + 3 * DC)
                k.op(ACT, lambda: nc.scalar.activation(out=X32[:, c, :], in_=t1[:, :], func=AF.Identity,
                                                       bias=lnp_t[:, bo + c:bo + c + 1], scale=lnp_t[:, go + c:go + c + 1]),
                     reads=[t1b, cstb], writes=[X32b[c]])
                if not final:
                    k.op(POOL, lambda: nc.gpsimd.tensor_scalar(out=XB[:, c, :], in0=t1[:, :], scalar1=lnp_t[:, base + c:base + c + 1],
                                                               scalar2=lnp_t[:, base + DC + c:base + DC + c + 1], op0=ALU.mult, op1=ALU.add),
                         reads=[t1b, cstb], writes=[XBb[c]])

        uid = [0]

        def mlp(li):
            uid[0] += 1
            with contextlib.ExitStack() as ms:
                A_t = [ms.enter_context(nc.sbuf_tensor(f"Ahid{i}_{uid[0]}", [P, 8, T], BF16)) for i in range(2)]
                A_b = [k.buf(f"Ahid{i}") for i in range(2)]
                for hb in range(16):
                    At, Ab = A_t[hb % 2], A_b[hb % 2]
                    for jj in range(8):
                        bank, bb = proj("m1", DC, lambda c: XB[:, c, :], lambda c: XBb[c])
                        r, rb = tmpA.next()
                        k.op(ACT, lambda: nc.scalar.activation(out=r[:, :], in_=bank[:, :], func=AF.Relu), reads=[bb], writes=[rb])
                        k.op(POOL, lambda: nc.gpsimd.tensor_tensor(out=At[:, jj, :], in0=r[:, :], in1=r[:, :], op=ALU.mult),
                             reads=[rb], writes=[Ab])
                    for g in range(8):
                        wsl = ws.next("m2")
                        for oc in range(4):
                            o = g * 4 + oc
                            bank, bb = proj("m2", 8, lambda c: At[:, c, :], lambda c: Ab, wsl=wsl, col0=oc * P)
                            k.op(DVE, lambda: nc.vector.tensor_tensor(out=X32[:, o, :], in0=bank[:, :], in1=X32[:, o, :], op=ALU.add),
                                 reads=[bb, X32b[o]], writes=[X32b[o]])
                k.scope_end(A_b)

        def load_tile(ti):
            n = 0
            for sb in range(4):
                r0 = ti * T + sb * P
                for g in range(8):
                    hf = 0
                    n += 1
                    k.dma(SP, io_ch[n % 2], IO[:, hf * T:(hf + 1) * T], x_d[r0:r0 + P, g * T:(g + 1) * T], writes=[IOb[hf]])
                    if "L1" in dbg:
                        continue
                    bank, bb = pmisc.next()
                    for cc in range(4):
                        k.op(PE, lambda: nc.tensor.transpose(out=bank[:, cc * P:(cc + 1) * P], in_=IO[:, hf * T + cc * P:hf * T + (cc + 1) * P],
                                                             identity=cst["ident_f"][:, :]),
                             reads=[IOb[hf], cstb], writes=[bb], mark=(cc == 3))
                    src = bank[:, :].rearrange("p (a b) -> p a b", a=4)
                    k.op(ACT, lambda: nc.scalar.activation(out=X32[:, g * 4:(g + 1) * 4, sb * P:(sb + 1) * P], in_=src, func=AF.Copy, scale=ALPHA),
                         reads=[bb], writes=[X32b[g * 4 + i] for i in range(4)])
                    k.op(POOL, lambda: nc.gpsimd.tensor_scalar_mul(out=XB[:, g * 4:(g + 1) * 4, sb * P:(sb + 1) * P],
                                                                   in0=X32[:, g * 4:(g + 1) * 4, sb * P:(sb + 1) * P], scalar1=1.0 / ALPHA),
                         reads=[X32b[g * 4 + i] for i in range(4)], writes=[XBb[g * 4 + i] for i in range(4)])

        def store_tile(ti):
            n = 0
            for sb in range(4):
                r0 = ti * T + sb * P
                for g in range(8):
                    hf = 0
                    n += 1
                    bank, bb = pmisc.next()
                    for cc in range(4):
                        c = g * 4 + cc
                        k.op(PE, lambda: nc.tensor.transpose(out=bank[:, cc * P:(cc + 1) * P], in_=X32[:, c, sb * P:(sb + 1) * P],
                                                             identity=cst["ident_f"][:, :]),
                             reads=[X32b[c], cstb], writes=[bb], mark=(cc == 3))
                    k.op(ACT, lambda: nc.scalar.activation(out=IO[:, 0:T], in_=bank[:, :], func=AF.Copy), reads=[bb], writes=[IOb[0]])
                    k.dma(SP, out_ch[n % 2], out_d[r0:r0 + P, g * T:(g + 1) * T], IO[:, hf * T:(hf + 1) * T], reads=[IOb[hf]])

        env = dict(k=k, nc=nc, W=W, DI=DI, cst=cst, cstb=cstb, ws=ws, X32=X32, XB=XB, X32b=X32b, XBb=XBb, proj=proj,
                   tmpA=tmpA, tmpB=tmpB, stat_m=stat_m, stat_r=stat_r, stat_b=stat_b, stats_add=stats_add,
                   stats_finish=stats_finish, pacc=pacc, pmisc=pmisc, load_vecT=load_vecT, layers=layers, nt=nt,
                   pos_d=pos_d, ret_state_d=ret_state_d, hg_state_d=hg_state_d, st_ch=st_ch, misc_ch=misc_ch,
                   hg_o_d=hg_o_d, hgob=k.buf('hg_o_dram'), IO=IO, IOb=IOb, io_ch=io_ch, psT=psT, psTb=psTb, ps_s2=ps_s2, ps_s1=ps_s1)
        mix = {}
        if do_mixer:
            if 0 in kinds:
                mix[0] = make_conv(env)
            if 1 in kinds:
                mix[1] = make_ret(env)
            if 2 in kinds:
                mix[2] = make_hgrn(env)
        if ws.n1 > 0 and "nomlp" not in dbg:
            ws.prologue()
        for ti in range(nt):
            if "noload" not in dbg:
                load_tile(ti)
            for n_l, li in enumerate(layers):
                last = (n_l == len(layers) - 1)
                if do_mixer:
                    mix[li % 3](li, ti)
                    layer_norm("ln_mix", li, final=(last and not do_mlp))
                if do_mlp:
                    if "nomlp" not in dbg:
                        mlp(li)
                    if "noln" not in dbg:
                        layer_norm("ln_mlp", li, final=last)
            if "nostore" not in dbg:
                store_tile(ti)
        assert ws.i == ws.total or "nomlp" in dbg, (ws.i, ws.total)
        k.finish(chans)
    return nc, consts


def make_conv(env):
    k, nc, W, cst, cstb, ws = env["k"], env["nc"], env["W"], env["cst"], env["cstb"], env["ws"]
    X32, XB, X32b, XBb, proj = env["X32"], env["XB"], env["X32b"], env["XBb"], env["proj"]
    tmpA, tmpB, stat_m, stat_r, stat_b = env["tmpA"], env["tmpB"], env["stat_m"], env["stat_r"], env["stat_b"]
    PE, ACT, DVE, POOL = k.PE, k.ACT, k.DVE, k.POOL
    conv_layers = [li for li in env["layers"] if li % 3 == 0]
    halo, halob, cv = {}, {}, {}
    for li in conv_layers:
        j = li // 3
        halo[li] = k.sbuf(f"halo{li}", [P, DC, HALO], F32)
        halob[li] = k.buf(f"halo{li}")
        k.op(POOL, lambda: nc.gpsimd.memset(halo[li][:, :, :], 0.0), writes=[halob[li]])
        vt = k.sbuf(f"cvec{li}", [P, 64 + CONV_W * DC + 4 * DC], F32)
        env["load_vecT"](vt, 0, W["conv_b_in"][j].rearrange("(c p) -> c p", p=P), 64)
        env["load_vecT"](vt, 64, W["conv_w_dw"][j].rearrange("k (c p) -> (k c) p", p=P), CONV_W * DC)
        o = 64 + CONV_W * DC
        for n_, nm in enumerate(("conv_b_dw", "conv_ln_g", "conv_ln_b", "conv_b_out")):
            env["load_vecT"](vt, o + n_ * DC, W[nm][j].rearrange("(c p) -> c p", p=P), DC)
        cv[li] = vt

    def conv(li, ti):
        vt = cv[li]
        o_dw = 64 + CONV_W * DC
        with contextlib.ExitStack() as ms:
            CB = ms.enter_context(nc.sbuf_tensor(f"CB_{li}_{ti}", [P, DC, T], BF16))
            CBb = [k.buf(f"CB{c}") for c in range(DC)]
            U = [ms.enter_context(nc.sbuf_tensor(f"U{i}_{li}_{ti}", [P, HALO + T], F32)) for i in range(2)]
            Ub = [k.buf(f"U{i}") for i in range(2)]
            st = {"n": 0}
            for c in range(DC):
                u, ub = U[c % 2], Ub[c % 2]
                ba, bba = proj("cin_a", DC, lambda kk: XB[:, kk, :], lambda kk: XBb[kk])
                bg, bbg = proj("cin_g", DC, lambda kk: XB[:, kk, :], lambda kk: XBb[kk])
                sg, sgb = tmpA.next()
                k.op(ACT, lambda: nc.scalar.activation(out=sg[:, :], in_=bg[:, :], func=AF.Sigmoid, bias=vt[:, DC + c:DC + c + 1], scale=1.0),
                     reads=[bbg, cstb], writes=[sgb])
                k.op(POOL, lambda: nc.gpsimd.tensor_copy(out=u[:, 0:HALO], in_=halo[li][:, c, :]), reads=[halob[li]], writes=[ub])
                k.op(DVE, lambda: nc.vector.scalar_tensor_tensor(out=u[:, HALO:HALO + T], in0=ba[:, :], scalar=vt[:, c:c + 1], in1=sg[:, :],
                                                                 op0=ALU.add, op1=ALU.mult),
                     reads=[bba, sgb, cstb], writes=[ub])
                k.op(POOL, lambda: nc.gpsimd.tensor_copy(out=halo[li][:, c, :], in_=u[:, T:T + HALO]), reads=[ub], writes=[halob[li]])
                acc, accb = tmpA.next()
                k.op(DVE, lambda: nc.vector.tensor_scalar(out=acc[:, :], in0=u[:, 0:T], scalar1=vt[:, 64 + c:64 + c + 1],
                                                          scalar2=vt[:, o_dw + c:o_dw + c + 1], op0=ALU.mult, op1=ALU.add),
                     reads=[ub, cstb], writes=[accb])
                for tap in range(1, CONV_W):
                    wc = 64 + tap * DC + c
                    k.op(DVE, lambda: nc.vector.scalar_tensor_tensor(out=acc[:, :], in0=u[:, tap:tap + T], scalar=vt[:, wc:wc + 1], in1=acc[:, :],
                                                                     op0=ALU.mult, op1=ALU.add),
                         reads=[ub, accb, cstb], writes=[accb])
                env["stats_add"](st, acc[:, :], accb, DC)
                k.op(POOL, lambda: nc.gpsimd.tensor_copy(out=CB[:, c, :], in_=acc[:, :]), reads=[accb], writes=[CBb[c]])
            env["stats_finish"](D, LN_EPS)
            for c in range(DC):
                t1, t1b = tmpA.next()
                k.op(DVE, lambda: nc.vector.tensor_tensor(out=t1[:, :], in0=CB[:, c, :], in1=stat_r[:, :], op=ALU.mult),
                     reads=[CBb[c], stat_b], writes=[t1b])
                k.op(DVE, lambda: nc.vector.tensor_tensor(out=t1[:, :], in0=t1[:, :], in1=stat_m[:, :], op=ALU.add),
                     reads=[t1b, stat_b], writes=[t1b])
                k.op(ACT, lambda: nc.scalar.activation(out=CB[:, c, :], in_=t1[:, :], func=AF.Silu,
                                                       bias=vt[:, o_dw + 2 * DC + c:o_dw + 2 * DC + c + 1],
                                                       scale=vt[:, o_dw + DC + c:o_dw + DC + c + 1]),
                     reads=[t1b, cstb], writes=[CBb[c]])
            for o in range(DC):
                bank, bb = proj("cout", DC, lambda kk: CB[:, kk, :], lambda kk: CBb[kk])
                k.op(DVE, lambda: nc.vector.scalar_tensor_tensor(out=X32[:, o, :], in0=bank[:, :], scalar=vt[:, o_dw + 3 * DC + o:o_dw + 3 * DC + o + 1],
                                                                 in1=X32[:, o, :], op0=ALU.add, op1=ALU.add),
                     reads=[bb, X32b[o], cstb], writes=[X32b[o]])
            k.scope_end(CBb + Ub)
    return conv


def make_ret(env):
    k, nc, W, cst, cstb, ws = env["k"], env["nc"], env["W"], env["cst"], env["cstb"], env["ws"]
    X32, XB, X32b, XBb, proj = env["X32"], env["XB"], env["X32b"], env["XBb"], env["proj"]
    tmpA, tmpB, pacc, pmisc, psT, psTb, ps_s2 = env["tmpA"], env["tmpB"], env["pacc"], env["pmisc"], env["psT"], env["psTb"], env["ps_s2"]
    PE, ACT, DVE, POOL, SP = k.PE, k.ACT, k.DVE, k.POOL, k.SP
    pos_d, state_d, st_ch = env["pos_d"], env["ret_state_d"], env["st_ch"]
    TWO_PI = float(2.0 * np.pi)

    def ret(li, ti):
        with contextlib.ExitStack() as ms:
            sbufs = []

            def alloc(name, shape, dt):
                b_ = k.buf(name)
                sbufs.append(b_)
                return ms.enter_context(nc.sbuf_tensor(f"{name}_{li}_{ti}", shape, dt)), b_
            cosT, csb = alloc("cosT", [P, T], F32)
            sinT, _ = alloc("sinT", [P, T], F32)
            qr, qrb = alloc("r_qr", [P, 2, T], BF16)
            qd, qdb = alloc("r_qd", [P, 2, T], BF16)
            kr, krb = alloc("r_kr", [P, 2, T], BF16)
            vT, vTb = alloc("r_vT", [P, 4, T], BF16)
            vtok, vtokb = alloc("r_vtok", [P, 4, T], BF16)
            kdtok, kdtokb = alloc("r_kdtok", [P, 4, 2 * P], BF16)
            S32, S32b = alloc("r_S32", [P, 2, T], F32)
            Sb, Sbb = alloc("r_Sb", [P, 2, T], BF16)
            Oc, Ocb = alloc("r_Oc", [P, 4, P], F32)
            Gs, Gsb = alloc("r_Gs", [P, 4, T], BF16)
            Zh, Zhb = vT, vTb
            intra, intrab = alloc("r_intra", [P, P], F32)
            qdec, qdecb = alloc("r_qdec", [P, T], F32)
            sT, sTb = alloc("r_sT", [P, P], BF16)
            rinv, rinvb = alloc("r_rinv", [P, P], F32)
            ni, nib = alloc("r_ni", [P, T], I32)
            posi, posib = ni[0:1, :], nib
            posf_t, posfb = tmpB.next()
            posf = posf_t[0:1, :]
            k.dma(SP, st_ch[0], posi, pos_d[0:1, ti * T:(ti + 1) * T], writes=[posib])
            k.op(DVE, lambda: nc.vector.tensor_copy(out=posf, in_=posi), reads=[posib], writes=[posfb])
            bank, bb = pmisc.next()
            k.op(PE, lambda: nc.tensor.matmul(bank[:, :], lhsT=cst["ones_f"][0:1, :], rhs=posf, start=True, stop=True),
                 reads=[posfb, cstb], writes=[bb])
            ang, angb = tmpA.next()
            k.op(DVE, lambda: nc.vector.tensor_scalar_mul(out=ang[:, :], in0=bank[:, :], scalar1=cst["inv_freq"][:, 0:1]),
                 reads=[bb, cstb], writes=[angb])
            PI2_HI = float(np.float32(2.0 * np.pi))
            PI2_LO = float(2.0 * np.pi - np.float64(np.float32(2.0 * np.pi)))

            def sin_of(dst, shift):
                a2, a2b = tmpB.next()
                k.op(DVE, lambda: nc.vector.tensor_scalar_add(out=a2[:, :], in0=ang[:, :], scalar1=shift), reads=[angb], writes=[a2b])
                r, rb = tmpB.next()
                k.op(DVE, lambda: nc.vector.tensor_scalar_mul(out=r[:, :], in0=a2[:, :], scalar1=float(1.0 / (2.0 * np.pi))), reads=[a2b], writes=[rb])
                k.op(DVE, lambda: nc.vector.tensor_copy(out=ni[:, :], in_=r[:, :]), reads=[rb], writes=[nib])
                k.op(DVE, lambda: nc.vector.tensor_copy(out=r[:, :], in_=ni[:, :]), reads=[nib], writes=[rb])
                k.op(DVE, lambda: nc.vector.scalar_tensor_tensor(out=a2[:, :], in0=r[:, :], scalar=-PI2_HI, in1=a2[:, :], op0=ALU.mult, op1=ALU.add),
                     reads=[rb, a2b], writes=[a2b])
                k.op(DVE, lambda: nc.vector.scalar_tensor_tensor(out=a2[:, :], in0=r[:, :], scalar=-PI2_LO, in1=a2[:, :], op0=ALU.mult, op1=ALU.add),
                     reads=[rb, a2b], writes=[a2b])
                k.op(DVE, lambda: nc.vector.tensor_single_scalar(out=r[:, :], in_=a2[:, :], scalar=float(np.pi), op=ALU.is_gt), reads=[a2b], writes=[rb])
                k.op(DVE, lambda: nc.vector.scalar_tensor_tensor(out=a2[:, :], in0=r[:, :], scalar=-PI2_HI, in1=a2[:, :], op0=ALU.mult, op1=ALU.add),
                     reads=[rb, a2b], writes=[a2b])
                k.op(DVE, lambda: nc.vector.tensor_scalar(out=a2[:, :], in0=a2[:, :], scalar1=float(np.pi), scalar2=-float(np.pi), op0=ALU.min, op1=ALU.max),
                     reads=[a2b], writes=[a2b])
                k.op(ACT, lambda: nc.scalar.activation(out=dst[:, :], in_=a2[:, :], func=AF.Sin), reads=[a2b], writes=[csb])

            sin_of(sinT, 0.0)
            sin_of(cosT, float(np.pi / 2))

            def rotary(dst, dstb, scale):
                raw = []
                for cc in range(2):
                    bank, bb = proj(key, DC, lambda kk: XB[:, kk, :], lambda kk: XBb[kk])
                    r, rb = (tmpA if cc == 0 else tmpB).next()
                    k.op(ACT, lambda: nc.scalar.activation(out=r[:, :], in_=bank[:, :], func=AF.Copy, scale=scale), reads=[bb], writes=[rb])
                    raw.append((r, rb))
                (q1, q1b), (q2, q2b) = raw
                t1, t1b = tmpA.next()
                t2, t2b = tmpB.next()
                k.op(DVE, lambda: nc.vector.tensor_tensor(out=t1[:, :], in0=q1[:, :], in1=cosT[:, :], op=ALU.mult), reads=[q1b, csb], writes=[t1b])
                k.op(DVE, lambda: nc.vector.tensor_tensor(out=t2[:, :], in0=q2[:, :], in1=sinT[:, :], op=ALU.mult), reads=[q2b, csb], writes=[t2b])
                k.op(DVE, lambda: nc.vector.tensor_tensor(out=dst[:, 0, :], in0=t1[:, :], in1=t2[:, :], op=ALU.subtract), reads=[t1b, t2b], writes=[dstb])
                k.op(DVE, lambda: nc.vector.tensor_tensor(out=t1[:, :], in0=q2[:, :], in1=cosT[:, :], op=ALU.mult), reads=[q2b, csb], writes=[t1b])
                k.op(DVE, lambda: nc.vector.tensor_tensor(out=t2[:, :], in0=q1[:, :], in1=sinT[:, :], op=ALU.mult), reads=[q1b, csb], writes=[t2b])
                k.op(DVE, lambda: nc.vector.tensor_tensor(out=dst[:, 1, :], in0=t1[:, :], in1=t2[:, :], op=ALU.add), reads=[t1b, t2b], writes=[dstb])

            for h in range(RET_H):
                lg = RET_LG[h]
                cdec = float(np.exp(lg * P))
                k.op(ACT, lambda: nc.scalar.activation(out=intra[:, :], in_=cst["diffT"][:, :], func=AF.Exp, scale=lg), reads=[cstb], writes=[intrab])
                k.op(DVE, lambda: nc.vector.tensor_tensor(out=intra[:, :], in0=intra[:, :], in1=cst["causT"][:, :], op=ALU.mult),
                     reads=[intrab, cstb], writes=[intrab])
                k.op(ACT, lambda: nc.scalar.activation(out=qdec[:, :], in_=cst["tp1"][:, :], func=AF.Exp, scale=lg), reads=[cstb], writes=[qdecb])
                if ti == 0:
                    k.op(POOL, lambda: nc.gpsimd.memset(S32[:, :, :], 0.0), writes=[S32b])
                else:
                    k.dma(SP, st_ch[0], S32[:, :, :], state_d[h].rearrange("p (a b) -> p a b", a=2), writes=[S32b])
                k.op(POOL, lambda: nc.gpsimd.tensor_copy(out=Sb[:, :, :], in_=S32[:, :, :]), reads=[S32b], writes=[Sbb])
                key = "rq"
                rotary(qr, qrb, 1.0)
                for cc in range(2):
                    k.op(DVE, lambda: nc.vector.tensor_tensor(out=qd[:, cc, :], in0=qr[:, cc, :], in1=qdec[:, :], op=ALU.mult),
                         reads=[qrb, qdecb], writes=[qdb])
                key = "rk"
                rotary(kr, krb, 1.0 / 16.0)
                for n in range(4):
                    for cc in range(2):
                        col = (n * 2 + cc) * P
                        k.op(PE, lambda: nc.tensor.transpose(out=psT[:, col:col + P], in_=kr[:, cc, n * P:(n + 1) * P], identity=cst["ident_b"][:, :]),
                             reads=[krb, cstb], writes=[psTb[0]], mark=(n == 3 and cc == 1))
                k.op(DVE, lambda: nc.vector.tensor_scalar_mul(out=kdtok[:, :, :].rearrange("p a b -> p (a b)"), in0=psT[:, :],
                                                              scalar1=cst["ret_kdec"][:, h:h + 1]),
                     reads=[psTb[0], cstb], writes=[kdtokb])
                for vc in range(4):
                    bank, bb = proj("rv", DC, lambda kk: XB[:, kk, :], lambda kk: XBb[kk])
                    k.op(ACT, lambda: nc.scalar.activation(out=vT[:, vc, :], in_=bank[:, :], func=AF.Copy), reads=[bb], writes=[vTb])
                for n in range(4):
                    hf = n % 2
                    for vc in range(4):
                        k.op(PE, lambda: nc.tensor.transpose(out=psT[:, hf * T + vc * P:hf * T + (vc + 1) * P], in_=vT[:, vc, n * P:(n + 1) * P],
                                                             identity=cst["ident_b"][:, :]),
                             reads=[vTb, cstb], writes=[psTb[hf]], mark=(vc == 3))
                    k.op(ACT, lambda: nc.scalar.activation(out=vtok[:, n, :], in_=psT[:, hf * T:(hf + 1) * T], func=AF.Copy),
                         reads=[psTb[hf]], writes=[vtokb])
                for vc in range(4):
                    bank, bb = proj("rg", DC, lambda kk: XB[:, kk, :], lambda kk: XBb[kk])
                    k.op(ACT, lambda: nc.scalar.activation(out=Gs[:, vc, :], in_=bank[:, :], func=AF.Silu), reads=[bb], writes=[Gsb])
                for n in range(4):
                    tr = slice(n * P, (n + 1) * P)
                    bank, bb = pacc.next()
                    for cc in range(2):
                        k.op(PE, lambda: nc.tensor.matmul(bank[:, 0:P], lhsT=kr[:, cc, tr], rhs=qr[:, cc, tr], start=(cc == 0), stop=(cc == 1)),
                             reads=[krb, qrb], writes=[bb], mark=(cc == 1))
                    k.op(DVE, lambda: nc.vector.tensor_tensor(out=sT[:, :], in0=bank[:, 0:P], in1=intra[:, :], op=ALU.mult),
                         reads=[bb, intrab], writes=[sTb])
                    bank2, bb2 = pacc.next()
                    for vc in range(4):
                        vs = slice(vc * P, (vc + 1) * P)
                        k.op(PE, lambda: nc.tensor.matmul(bank2[:, vs], lhsT=vtok[:, n, vs], rhs=sT[:, :], start=True, stop=False),
                             reads=[vtokb, sTb], writes=[bb2], mark=False)
                        k.op(PE, lambda: nc.tensor.matmul(bank2[:, vs], lhsT=Sb[:, 0, vs], rhs=qd[:, 0, tr], start=False, stop=False),
                             reads=[Sbb, qdb], writes=[bb2], mark=False)
                        k.op(PE, lambda: nc.tensor.matmul(bank2[:, vs], lhsT=Sb[:, 1, vs], rhs=qd[:, 1, tr], start=False, stop=True),
                             reads=[Sbb, qdb], writes=[bb2], mark=(vc == 3))
                    k.op(ACT, lambda: nc.scalar.activation(out=Oc[:, :, :].rearrange("p a b -> p (a b)"), in_=bank2[:, :], func=AF.Copy),
                         reads=[bb2], writes=[Ocb])
                    for dc in range(2):
                        bank3, bb3 = pacc.next()
                        k.op(PE, lambda: nc.tensor.matmul(bank3[:, :], lhsT=kdtok[:, n, dc * P:(dc + 1) * P], rhs=vtok[:, n, :], start=True, stop=True),
                             reads=[kdtokb, vtokb], writes=[bb3])
                        k.op(DVE, lambda: nc.vector.scalar_tensor_tensor(out=S32[:, dc, :], in0=S32[:, dc, :], scalar=cdec, in1=bank3[:, :],
                                                                         op0=ALU.mult, op1=ALU.add),
                             reads=[S32b, bb3], writes=[S32b])
                    k.op(POOL, lambda: nc.gpsimd.tensor_copy(out=Sb[:, :, :], in_=S32[:, :, :]), reads=[S32b], writes=[Sbb])
                    for vc in range(4):
                        sq, sqb = tmpB.next()
                        k.op(ACT, lambda: nc.scalar.activation(out=sq[:, 0:P], in_=Oc[:, vc, :], func=AF.Square), reads=[Ocb], writes=[sqb])
                        k.op(PE, lambda: nc.tensor.matmul(ps_s2[0][:, 0:P], lhsT=cst["ones_f"][:, :], rhs=sq[:, 0:P], start=(vc == 0), stop=(vc == 3)),
                             reads=[sqb, cstb], writes=[ps_s2[1]])
                    tq, tqb = tmpA.next()
                    k.op(DVE, lambda: nc.vector.tensor_scalar(out=tq[:, 0:P], in0=ps_s2[0][:, 0:P], scalar1=1.0 / 512.0, scalar2=RMS_EPS,
                                                              op0=ALU.mult, op1=ALU.add),
                         reads=[ps_s2[1]], writes=[tqb])
                    k.op(DVE, lambda: nc.vector.reciprocal(out=tq[:, 0:P], in_=tq[:, 0:P]), reads=[tqb], writes=[tqb])
                    k.op(ACT, lambda: nc.scalar.activation(out=rinv[:, :], in_=tq[:, 0:P], func=AF.Sqrt), reads=[tqb], writes=[rinvb])
                    for vc in range(4):
                        z1, z1b = tmpB.next()
                        k.op(DVE, lambda: nc.vector.tensor_tensor(out=z1[:, 0:P], in0=Oc[:, vc, :], in1=rinv[:, :], op=ALU.mult),
                             reads=[Ocb, rinvb], writes=[z1b])
                        k.op(DVE, lambda: nc.vector.tensor_tensor(out=Zh[:, vc, tr], in0=z1[:, 0:P], in1=Gs[:, vc, tr], op=ALU.mult),
                             reads=[z1b, Gsb], writes=[Zhb])
                if ti < env["nt"] - 1:
                    k.dma(SP, st_ch[1], state_d[h].rearrange("p (a b) -> p a b", a=2), S32[:, :, :], reads=[S32b])
                for g4 in range(4):
                    wsl = ws.next("ro")
                    for oc in range(8):
                        o = g4 * 8 + oc
                        bank, bb = proj("ro", 4, lambda kk: Zh[:, kk, :], lambda kk: Zhb, wsl=wsl, col0=oc * P)
                        k.op(DVE, lambda: nc.vector.tensor_tensor(out=X32[:, o, :], in0=bank[:, :], in1=X32[:, o, :], op=ALU.add),
                             reads=[bb, X32b[o]], writes=[X32b[o]])
            k.scope_end(sbufs)
    return ret


def make_hgrn(env):
    k, nc, W, cst, cstb, ws = env["k"], env["nc"], env["W"], env["cst"], env["cstb"], env["ws"]
    X32, XB, X32b, XBb, proj = env["X32"], env["XB"], env["X32b"], env["XBb"], env["proj"]
    tmpA, tmpB, pacc, pmisc, psT, psTb, ps_s2 = env["tmpA"], env["tmpB"], env["pacc"], env["pmisc"], env["psT"], env["psTb"], env["ps_s2"]
    stat_r, stat_b = env["stat_r"], env["stat_b"]
    PE, ACT, DVE, POOL, SP = k.PE, k.ACT, k.DVE, k.POOL, k.SP
    state_d, o_d, st_ch = env["hg_state_d"], env["hg_o_d"], env["st_ch"]
    hg_layers = [li for li in env["layers"] if li % 3 == 2]
    LBR = k.sbuf("hg_lbr", [P, DEPTH * DC], F32)
    env["load_vecT"](LBR, 0, W["hgrn_lower_bounds"].rearrange("l (c p) -> (l c) p", p=P), DEPTH * DC)
    mx = k.sbuf("hg_mx", [P, DC], F32)
    sm = k.sbuf("hg_sm", [P, DC], F32)
    k.op(DVE, lambda: nc.vector.tensor_tensor(out=mx[:, :], in0=LBR[:, 0:DC], in1=LBR[:, DC:2 * DC], op=ALU.max), reads=[cstb], writes=[cstb])
    for l in range(2, DEPTH):
        k.op(DVE, lambda: nc.vector.tensor_tensor(out=mx[:, :], in0=mx[:, :], in1=LBR[:, l * DC:(l + 1) * DC], op=ALU.max), reads=[cstb], writes=[cstb])
    for l in range(DEPTH):
        k.op(DVE, lambda: nc.vector.tensor_tensor(out=LBR[:, l * DC:(l + 1) * DC], in0=LBR[:, l * DC:(l + 1) * DC], in1=mx[:, :], op=ALU.subtract),
             reads=[cstb], writes=[cstb])
    k.op(ACT, lambda: nc.scalar.activation(out=LBR[:, :], in_=LBR[:, :], func=AF.Exp), reads=[cstb], writes=[cstb])
    k.op(DVE, lambda: nc.vector.tensor_tensor(out=sm[:, :], in0=LBR[:, 0:DC], in1=LBR[:, DC:2 * DC], op=ALU.add), reads=[cstb], writes=[cstb])
    for l in range(2, DEPTH):
        k.op(DVE, lambda: nc.vector.tensor_tensor(out=sm[:, :], in0=sm[:, :], in1=LBR[:, l * DC:(l + 1) * DC], op=ALU.add), reads=[cstb], writes=[cstb])
    k.op(DVE, lambda: nc.vector.reciprocal(out=sm[:, :], in_=sm[:, :]), reads=[cstb], writes=[cstb])
    OML, NG = {}, {}
    for li in hg_layers:
        lbt = k.sbuf(f"hg_lb{li}", [P, DC], F32)
        k.op(DVE, lambda: nc.vector.tensor_copy(out=lbt[:, :], in_=LBR[:, DC:2 * DC]), reads=[cstb], writes=[cstb])
        for l in range(2, li + 1):
            k.op(DVE, lambda: nc.vector.tensor_tensor(out=lbt[:, :], in0=lbt[:, :], in1=LBR[:, l * DC:(l + 1) * DC], op=ALU.add), reads=[cstb], writes=[cstb])
        k.op(DVE, lambda: nc.vector.tensor_tensor(out=lbt[:, :], in0=lbt[:, :], in1=sm[:, :], op=ALU.mult), reads=[cstb], writes=[cstb])
        k.op(DVE, lambda: nc.vector.tensor_scalar(out=lbt[:, :], in0=lbt[:, :], scalar1=-1.0, scalar2=1.0, op0=ALU.mult, op1=ALU.add),
             reads=[cstb], writes=[cstb])
        OML[li] = lbt
        ng = k.sbuf(f"hg_ng{li}", [P, DC], F32)
        env["load_vecT"](ng, 0, W["hgrn_norm_g"][li // 3].rearrange("(c p) -> c p", p=P), DC)
        NG[li] = ng

    def hgrn(li, ti):
        oml, ng = OML[li], NG[li]
        with contextlib.ExitStack() as ms:
            sbufs = []

            def alloc(name, shape, dt):
                b_ = k.buf(name)
                sbufs.append(b_)
                return ms.enter_context(nc.sbuf_tensor(f"{name}_{li}_{ti}", shape, dt)), b_
            qs, qsb = alloc("h_qs", [P, T], F32)
            kf, kfb = alloc("h_kf", [P, T], F32)
            bb_, bbb = alloc("h_b", [P, T], F32)
            Qs, Qsb = alloc("h_Qs", [P, T], BF16)
            Qb, Qbb = alloc("h_Qb", [P, T], BF16)
            Kb, Kbb = alloc("h_Kb", [P, T], BF16)
            Kd, Kdb = alloc("h_Kd", [P, T], BF16)
            iT, iTb = alloc("h_iT", [P, T], BF16)
            itok, itokb = alloc("h_itok", [P, T], BF16)
            kdtok, kdtokb = alloc("h_kdtok", [P, T], BF16)
            sT, sTb = alloc("h_sT", [P, P], BF16)
            S32, S32b = alloc("h_S32", [P, P], F32)
            Sb, Sbb = alloc("h_Sb", [P, P], BF16)
            ebl, eblb = alloc("h_ebl", [P, 16], F32)
            Ob, Obb = alloc("h_Ob", [P, T], BF16)
            O4, O4b = alloc("h_O4", [P, 4, T], BF16)
            Z4, Z4b = alloc("h_Z4", [P, 4, T], BF16)
            for h in range(HG_H):
                if ti == 0:
                    k.op(POOL, lambda: nc.gpsimd.memset(S32[:, :], 0.0), writes=[S32b])
                else:
                    k.dma(SP, st_ch[0], S32[:, :], state_d[h], writes=[S32b])
                k.op(POOL, lambda: nc.gpsimd.tensor_copy(out=Sb[:, :], in_=S32[:, :]), reads=[S32b], writes=[Sbb])
                bank, bk = proj("hq", DC, lambda kk: XB[:, kk, :], lambda kk: XBb[kk])
                k.op(ACT, lambda: nc.scalar.activation(out=qs[:, :], in_=bank[:, :], func=AF.Silu), reads=[bk], writes=[qsb])
                bank, bk = proj("hf", DC, lambda kk: XB[:, kk, :], lambda kk: XBb[kk])
                k.op(ACT, lambda: nc.scalar.activation(out=kf[:, :], in_=bank[:, :], func=AF.Sigmoid, scale=-1.0), reads=[bk], writes=[kfb])
                k.op(DVE, lambda: nc.vector.tensor_scalar_mul(out=kf[:, :], in0=kf[:, :], scalar1=oml[:, h:h + 1]), reads=[kfb, cstb], writes=[kfb])
                lf, lfb = tmpA.next()
                k.op(DVE, lambda: nc.vector.tensor_scalar(out=lf[:, :], in0=kf[:, :], scalar1=-1.0, scalar2=1.0, op0=ALU.mult, op1=ALU.add),
                     reads=[kfb], writes=[lfb])
                k.op(ACT, lambda: nc.scalar.activation(out=lf[:, :], in_=lf[:, :], func=AF.Ln), reads=[lfb], writes=[lfb])
                for m in range(4):
                    k.op(DVE, lambda: nc.vector.tensor_tensor_scan(out=bb_[:, m * P:(m + 1) * P], data0=cst["ones_f"][:, :], data1=lf[:, m * P:(m + 1) * P],
                                                                  initial=0.0, op0=ALU.mult, op1=ALU.add),
                         reads=[lfb, cstb], writes=[bbb])
                bank, bk = proj("hi", DC, lambda kk: XB[:, kk, :], lambda kk: XBb[kk])
                k.op(ACT, lambda: nc.scalar.activation(out=iT[:, :], in_=bank[:, :], func=AF.Copy), reads=[bk], writes=[iTb])
                for m in range(4):
                    k.op(PE, lambda: nc.tensor.transpose(out=psT[:, m * P:(m + 1) * P], in_=iT[:, m * P:(m + 1) * P], identity=cst["ident_b"][:, :]),
                         reads=[iTb, cstb], writes=[psTb[0]], mark=(m == 3))
                k.op(POOL if False else ACT, lambda: nc.scalar.activation(out=itok[:, :], in_=psT[:, 0:T], func=AF.Copy), reads=[psTb[0]], writes=[itokb])
                e1, e1b = tmpB.next()
                k.op(ACT, lambda: nc.scalar.activation(out=e1[:, :], in_=bb_[:, :], func=AF.Exp), reads=[bbb], writes=[e1b])
                k.op(DVE, lambda: nc.vector.tensor_tensor(out=Qs[:, :], in0=qs[:, :], in1=e1[:, :], op=ALU.mult), reads=[qsb, e1b], writes=[Qsb])
                for m in range(4):
                    k.op(ACT, lambda: nc.scalar.activation(out=ebl[:, m:m + 1], in_=bb_[:, m * P + P - 1:m * P + P], func=AF.Exp), reads=[bbb], writes=[eblb])
                e2, e2b = tmpB.next()
                for m in range(4):
                    k.op(DVE, lambda: nc.vector.tensor_scalar(out=e2[:, m * P:(m + 1) * P], in0=bb_[:, m * P:(m + 1) * P],
                                                              scalar1=bb_[:, m * P + P - 1:m * P + P], scalar2=-1.0, op0=ALU.subtract, op1=ALU.mult),
                         reads=[bbb], writes=[e2b])
                k.op(ACT, lambda: nc.scalar.activation(out=e2[:, :], in_=e2[:, :], func=AF.Exp), reads=[e2b], writes=[e2b])
                k.op(DVE, lambda: nc.vector.tensor_tensor(out=Kd[:, :], in0=kf[:, :], in1=e2[:, :], op=ALU.mult), reads=[kfb, e2b], writes=[Kdb])
                for m in range(4):
                    k.op(PE, lambda: nc.tensor.transpose(out=psT[:, T + m * P:T + (m + 1) * P], in_=Kd[:, m * P:(m + 1) * P], identity=cst["ident_b"][:, :]),
                         reads=[Kdb, cstb], writes=[psTb[0]], mark=(m == 3))
                k.op(ACT, lambda: nc.scalar.activation(out=kdtok[:, :], in_=psT[:, T:2 * T], func=AF.Copy), reads=[psTb[0]], writes=[kdtokb])
                bc, bcb = tmpA.next()
                for m in range(4):
                    k.op(DVE, lambda: nc.vector.tensor_scalar(out=bc[:, m * P:(m + 1) * P], in0=bb_[:, m * P:(m + 1) * P],
                                                              scalar1=bb_[:, m * P + 63:m * P + 64], scalar2=None, op0=ALU.subtract),
                         reads=[bbb], writes=[bcb])
                e3, e3b = tmpB.next()
                k.op(ACT, lambda: nc.scalar.activation(out=e3[:, :], in_=bc[:, :], func=AF.Exp), reads=[bcb], writes=[e3b])
                k.op(DVE, lambda: nc.vector.tensor_tensor(out=Qb[:, :], in0=qs[:, :], in1=e3[:, :], op=ALU.mult), reads=[qsb, e3b], writes=[Qbb])
                e4, e4b = tmpB.next()
                k.op(ACT, lambda: nc.scalar.activation(out=e4[:, :], in_=bc[:, :], func=AF.Exp, scale=-1.0), reads=[bcb], writes=[e4b])
                k.op(DVE, lambda: nc.vector.tensor_tensor(out=Kb[:, :], in0=kf[:, :], in1=e4[:, :], op=ALU.mult), reads=[kfb, e4b], writes=[Kbb])
                obank, obb = env['ps_s1']
                for m in range(4):
                    tr = slice(m * P, (m + 1) * P)
                    bank, bk = pacc.next()
                    k.op(PE, lambda: nc.tensor.matmul(bank[:, 0:P], lhsT=Kb[:, tr], rhs=Qb[:, tr], start=True, stop=True), reads=[Kbb, Qbb], writes=[bk])
                    k.op(DVE, lambda: nc.vector.tensor_tensor(out=sT[:, :], in0=bank[:, 0:P], in1=cst["causT"][:, :], op=ALU.mult),
                         reads=[bk, cstb], writes=[sTb])
                    k.op(PE, lambda: nc.tensor.matmul(obank[:, tr], lhsT=itok[:, tr], rhs=sT[:, :], start=True, stop=False),
                         reads=[itokb, sTb], writes=[obb], mark=False)
                    k.op(PE, lambda: nc.tensor.matmul(obank[:, tr], lhsT=Sb[:, :], rhs=Qs[:, tr], start=False, stop=True),
                         reads=[Sbb, Qsb], writes=[obb], mark=True)
                    bank3, bk3 = pacc.next()
                    k.op(PE, lambda: nc.tensor.matmul(bank3[:, 0:P], lhsT=kdtok[:, tr], rhs=itok[:, tr], start=True, stop=True),
                         reads=[kdtokb, itokb], writes=[bk3])
                    k.op(DVE, lambda: nc.vector.scalar_tensor_tensor(out=S32[:, :], in0=S32[:, :], scalar=ebl[:, m:m + 1], in1=bank3[:, 0:P],
                                                                     op0=ALU.mult, op1=ALU.add),
                         reads=[S32b, bk3, eblb], writes=[S32b])
                    k.op(POOL, lambda: nc.gpsimd.tensor_copy(out=Sb[:, :], in_=S32[:, :]), reads=[S32b], writes=[Sbb])
                if ti < env["nt"] - 1:
                    k.dma(SP, st_ch[1], state_d[h], S32[:, :], reads=[S32b])
                of, ofb = tmpA.next()
                k.op(ACT, lambda: nc.scalar.activation(out=of[:, :], in_=obank[:, :], func=AF.Copy), reads=[obb], writes=[ofb])
                sq, sqb = tmpB.next()
                k.op(ACT, lambda: nc.scalar.activation(out=sq[:, :], in_=of[:, :], func=AF.Square), reads=[ofb], writes=[sqb])
                k.op(PE, lambda: nc.tensor.matmul(ps_s2[0][:, :], lhsT=cst["ones_f"][:, :], rhs=sq[:, :], start=(h == 0), stop=(h == HG_H - 1)),
                     reads=[sqb, cstb], writes=[ps_s2[1]])
                k.op(POOL, lambda: nc.gpsimd.tensor_copy(out=Ob[:, :], in_=of[:, :]), reads=[ofb], writes=[Obb])
                k.dma(SP, st_ch[1], o_d[h], Ob[:, :], reads=[Obb], writes=[env["hgob"]])
            env["stats_finish"](D, RMS_EPS, mean=False)
            for h4 in range(8):
                k.dma(SP, st_ch[0], O4[:, :, :], o_d[h4 * 4:(h4 + 1) * 4].rearrange("h p t -> p h t"), reads=[env["hgob"]], writes=[O4b])
                for hh in range(4):
                    oc = h4 * 4 + hh
                    bank, bk = proj("hg", DC, lambda kk: XB[:, kk, :], lambda kk: XBb[kk])
                    gs, gsb = tmpA.next()
                    k.op(ACT, lambda: nc.scalar.activation(out=gs[:, :], in_=bank[:, :], func=AF.Silu), reads=[bk], writes=[gsb])
                    z1, z1b = tmpB.next()
                    k.op(DVE, lambda: nc.vector.tensor_tensor(out=z1[:, :], in0=O4[:, hh, :], in1=stat_r[:, :], op=ALU.mult),
                         reads=[O4b, stat_b], writes=[z1b])
                    k.op(DVE, lambda: nc.vector.scalar_tensor_tensor(out=Z4[:, hh, :], in0=z1[:, :], scalar=ng[:, oc:oc + 1], in1=gs[:, :],
                                                                     op0=ALU.mult, op1=ALU.mult),
                         reads=[z1b, gsb, cstb], writes=[Z4b])
                for g4 in range(4):
                    wsl = ws.next("ho")
                    for o8 in range(8):
                        o = g4 * 8 + o8
                        bank, bk = proj("ho", 4, lambda kk: Z4[:, kk, :], lambda kk: Z4b, wsl=wsl, col0=o8 * P)
                        k.op(DVE, lambda: nc.vector.tensor_tensor(out=X32[:, o, :], in0=bank[:, :], in1=X32[:, o, :], op=ALU.add),
                             reads=[bk, X32b[o]], writes=[X32b[o]])
            k.scope_end(sbufs)
    return hgrn


def make_in_map(inputs, b, nt, layers, consts):
    S = nt * T
    kinds = sorted(set(li % 3 for li in layers))
    m = {"x": np.ascontiguousarray(inputs["x"][b, :S]), "positions": np.ascontiguousarray(inputs["positions"][b:b + 1, :S])}
    names = ["mlp_w1", "mlp_w2", "ln_mix_g", "ln_mix_b", "ln_mlp_g", "ln_mlp_b", "hgrn_lower_bounds"]
    for kd in kinds:
        names += WEIGHT_NAMES[kd]
    for nm in names:
        m[nm] = inputs[nm]
    for nm in CONST_SHAPES:
        m["c_" + nm] = consts[nm]
    return m


def kernel(**inputs):
    nt = 8
    layers = [0, 1, 2, 3]
    nc, consts = build(nt, layers)
    inputs = {kk: np.asarray(v) for kk, v in inputs.items()}
    in_maps = [make_in_map(inputs, b, nt, layers, consts) for b in range(2)]
    res = run_bass_kernel_spmd(nc, in_maps, core_ids=[0, 1])
    return np.stack([res.results[b]["out"] for b in range(2)], axis=0).astype(np.float32)
```
